# Optimizing a Trainium2 kernel written in Bass

```python
import math
import numpy as np
import jax
import jax.numpy as jnp
from jax import lax

D_MODEL = 2048
BATCH = 16
SEQ = 2048
DEPTH = 1

SSD_D_INNER = D_MODEL
SSD_HEAD_DIM = 64
SSD_HEADS = SSD_D_INNER // SSD_HEAD_DIM
SSD_GROUPS = 4
SSD_D_STATE = 128
SSD_CONV_K = 4
SSD_CHUNK = 128
SSD_CONV_DIM = SSD_D_INNER + 2 * SSD_GROUPS * SSD_D_STATE

NSA_HEADS = 16
NSA_HEAD_DIM = 128
NSA_KV_GROUPS = 2
NSA_HPG = NSA_HEADS // NSA_KV_GROUPS
NSA_D = NSA_HEADS * NSA_HEAD_DIM
NSA_KV_D = NSA_KV_GROUPS * NSA_HEAD_DIM
N_NSA_BRANCHES = 3
CMP_BLOCK = 32
CMP_STRIDE = 16
SEL_BLOCK = 64
SEL_TOPK = 8
WINDOW = 512
Q_BLOCK = 128

IN_DIM = (SSD_D_INNER + SSD_CONV_DIM + SSD_HEADS
          + NSA_D + 6 * NSA_KV_D + NSA_D + NSA_HEADS * N_NSA_BRANCHES
          + 2 * D_MODEL)
EPS = 1e-6
NEG_INF = -1e30
FORCED_SCORE = 1e9

kernel_name = 'hybrid_ssd_nsa_gated_block'


def rms_norm(x, w):
    xf = x.astype(jnp.float32)
    xf = xf * lax.rsqrt(jnp.mean(xf * xf, axis=-1, keepdims=True) + EPS)
    return (xf * w.astype(jnp.float32)).astype(x.dtype)


def masked_softmax(s, mask):
    s = jnp.where(mask, s.astype(jnp.float32), NEG_INF)
    return jnp.where(mask, jax.nn.softmax(s, axis=-1), 0.0)


def causal_dwconv(u, w, b):
    k, c = w.shape
    y = lax.conv_general_dilated(u, w[:, None, :].astype(u.dtype), window_strides=(1,),
                                 padding=[(k - 1, 0)], dimension_numbers=('NWC', 'WIO', 'NWC'),
                                 feature_group_count=c)
    return y + b.astype(u.dtype)


def ssd_chunked(x, dt, a, bmat, cmat):
    b_, t_, h_, p_ = x.shape
    g_, n_ = bmat.shape[2], bmat.shape[3]
    hpg = h_ // g_
    q_ = SSD_CHUNK
    nc = t_ // q_
    xc = (x * dt[..., None].astype(x.dtype)).reshape(b_, nc, q_, g_, hpg, p_)
    bc = bmat.reshape(b_, nc, q_, g_, n_)
    cc = cmat.reshape(b_, nc, q_, g_, n_)
    la = (dt * a).reshape(b_, nc, q_, g_, hpg).transpose(0, 1, 3, 4, 2)
    la_cum = jnp.cumsum(la, axis=-1)
    causal = jnp.tril(jnp.ones((q_, q_), bool))
    seg = la_cum[..., :, None] - la_cum[..., None, :]
    lmat = jnp.exp(jnp.where(causal, seg, -jnp.inf)).astype(x.dtype)
    cb = jnp.einsum('bclgn,bcsgn->bcgls', cc, bc)
    y_diag = jnp.einsum('bcgls,bcgkls,bcsgkp->bclgkp', cb, lmat, xc)
    decay_to_end = jnp.exp(la_cum[..., -1:] - la_cum).astype(x.dtype)
    states = jnp.einsum('bcsgn,bcgks,bcsgkp->bcgkpn', bc, decay_to_end, xc)
    chunk_decay = jnp.exp(la_cum[..., -1])

    def step(h, inp):
        st, dec = inp
        return h * dec[..., None, None] + st.astype(jnp.float32), h

    h0 = jnp.zeros((b_, g_, hpg, p_, n_), jnp.float32)
    _, prev = lax.scan(step, h0, (jnp.moveaxis(states, 1, 0), jnp.moveaxis(chunk_decay, 1, 0)))
    prev = jnp.moveaxis(prev, 0, 1).astype(x.dtype)
    decay_from_start = jnp.exp(la_cum).astype(x.dtype)
    y_off = jnp.einsum('bclgn,bcgkpn,bcgkl->bclgkp', cc, prev, decay_from_start)
    return (y_diag + y_off).reshape(b_, t_, h_, p_)


def ssd_branch(z, xbc, dt_raw, conv_w, conv_b, dt_bias, a_log, d_skip, ssd_norm_w, w_out):
    b, t, _ = xbc.shape
    xbc = jax.nn.silu(causal_dwconv(xbc, conv_w, conv_b))
    xs, bs, cs = jnp.split(xbc, [SSD_D_INNER, SSD_D_INNER + SSD_GROUPS * SSD_D_STATE], axis=-1)
    xs = xs.reshape(b, t, SSD_HEADS, SSD_HEAD_DIM)
    dt = jax.nn.softplus(dt_raw.astype(jnp.float32) + dt_bias.astype(jnp.float32))
    a = -jnp.exp(a_log.astype(jnp.float32))
    y = ssd_chunked(xs, dt, a, bs.reshape(b, t, SSD_GROUPS, SSD_D_STATE),
                    cs.reshape(b, t, SSD_GROUPS, SSD_D_STATE))
    y = y + xs * d_skip[:, None].astype(xs.dtype)
    y = rms_norm(y.reshape(b, t, SSD_D_INNER) * jax.nn.silu(z), ssd_norm_w)
    return y @ w_out


def compress_blocks(kv, pe, w1, b1, w2):
    b, t, g, d = kv.shape
    n_cmp = (t - CMP_BLOCK) // CMP_STRIDE + 1
    idx = np.arange(n_cmp)[:, None] * CMP_STRIDE + np.arange(CMP_BLOCK)[None, :]
    blk = kv[:, idx] + pe[None, None, :, None, :]
    blk = blk.transpose(0, 1, 3, 2, 4).reshape(b, n_cmp, g, CMP_BLOCK * d)
    return jax.nn.silu(blk @ w1 + b1) @ w2


def nsa_branch(q, k_cmp, v_cmp, k_slc, v_slc, k_win, v_win, z, gate_logits,
               q_norm_w, k_cmp_norm_w, k_slc_norm_w, k_win_norm_w,
               cmp_pe_k, cmp_w1_k, cmp_b1_k, cmp_w2_k,
               cmp_pe_v, cmp_w1_v, cmp_b1_v, cmp_w2_v, w_out):
    b, t, _ = q.shape
    dtype = q.dtype
    G, K, HD = NSA_KV_GROUPS, NSA_HPG, NSA_HEAD_DIM
    scale = HD ** -0.5
    q = rms_norm(q.reshape(b, t, NSA_HEADS, HD), q_norm_w).reshape(b, t, G, K, HD)
    kv_shape = (b, t, G, HD)
    k_slc = rms_norm(k_slc.reshape(kv_shape), k_slc_norm_w)
    k_win = rms_norm(k_win.reshape(kv_shape), k_win_norm_w)
    v_slc = v_slc.reshape(kv_shape)
    v_win = v_win.reshape(kv_shape)
    kc = rms_norm(compress_blocks(k_cmp.reshape(kv_shape), cmp_pe_k, cmp_w1_k, cmp_b1_k, cmp_w2_k), k_cmp_norm_w)
    vc = compress_blocks(v_cmp.reshape(kv_shape), cmp_pe_v, cmp_w1_v, cmp_b1_v, cmp_w2_v)
    n_cmp = kc.shape[1]
    cmp_end = jnp.asarray(np.arange(n_cmp) * CMP_STRIDE + CMP_BLOCK - 1, jnp.int32)
    n_slc = t // SEL_BLOCK
    n_sel = min(SEL_TOPK, n_slc)
    ci = np.arange(n_cmp)[:, None] * CMP_STRIDE
    sj = np.arange(n_slc)[None, :] * SEL_BLOCK
    sel_map = jnp.asarray(((ci < sj + SEL_BLOCK) & (ci + CMP_BLOCK > sj)).astype(np.float32))
    ks_blocks = k_slc.reshape(b, n_slc, SEL_BLOCK, G, HD).transpose(0, 3, 1, 2, 4)
    vs_blocks = v_slc.reshape(b, n_slc, SEL_BLOCK, G, HD).transpose(0, 3, 1, 2, 4)
    blk_idx = jnp.arange(n_slc)
    b_idx = jnp.arange(b)[:, None, None, None]
    g_idx = jnp.arange(G)[None, :, None, None]
    pad = ((0, 0), (WINDOW, 0), (0, 0), (0, 0))
    k_win_p = jnp.pad(k_win, pad)
    v_win_p = jnp.pad(v_win, pad)
    gates = jax.nn.sigmoid(gate_logits.astype(jnp.float32)).astype(dtype).reshape(b, t, G, K, N_NSA_BRANCHES)

    def query_block(i):
        qs = i * Q_BLOCK
        tq = qs + jnp.arange(Q_BLOCK)
        qb = lax.dynamic_slice_in_dim(q, qs, Q_BLOCK, axis=1)
        s = jnp.einsum('bqgkd,bcgd->bgkqc', qb, kc) * scale
        p_cmp = masked_softmax(s, cmp_end[None, :] <= tq[:, None])
        o_cmp = jnp.einsum('bgkqc,bcgd->bqgkd', p_cmp.astype(dtype), vc)
        imp = jnp.einsum('bgkqc,cj->bgqj', p_cmp, sel_map)
        blk_valid = blk_idx[None, :] * SEL_BLOCK <= tq[:, None]
        forced = (blk_idx[None, :] == (tq // SEL_BLOCK)[:, None]) | (blk_idx[None, :] == 0)
        imp = jnp.where(forced, FORCED_SCORE, jnp.where(blk_valid, imp, -1.0))
        _, sel = lax.top_k(imp, n_sel)
        kg = ks_blocks[b_idx, g_idx, sel]
        vg = vs_blocks[b_idx, g_idx, sel]
        pos = sel[..., None] * SEL_BLOCK + jnp.arange(SEL_BLOCK)
        mask_s = (pos <= tq[None, None, :, None, None]).reshape(b, G, 1, Q_BLOCK, n_sel * SEL_BLOCK)
        s = jnp.einsum('bqgkd,bgqnld->bgkqnl', qb, kg).reshape(b, G, K, Q_BLOCK, n_sel * SEL_BLOCK) * scale
        p = masked_softmax(s, mask_s).astype(dtype).reshape(b, G, K, Q_BLOCK, n_sel, SEL_BLOCK)
        o_slc = jnp.einsum('bgkqnl,bgqnld->bqgkd', p, vg)
        kw = lax.dynamic_slice_in_dim(k_win_p, qs, Q_BLOCK + WINDOW, axis=1)
        vw = lax.dynamic_slice_in_dim(v_win_p, qs, Q_BLOCK + WINDOW, axis=1)
        kpos = qs - WINDOW + jnp.arange(Q_BLOCK + WINDOW)
        mask_w = ((kpos[None, :] <= tq[:, None]) & (kpos[None, :] > tq[:, None] - WINDOW)
                  & (kpos[None, :] >= 0))
        s = jnp.einsum('bqgkd,bsgd->bgkqs', qb, kw) * scale
        p = masked_softmax(s, mask_w).astype(dtype)
        o_win = jnp.einsum('bgkqs,bsgd->bqgkd', p, vw)
        gb = lax.dynamic_slice_in_dim(gates, qs, Q_BLOCK, axis=1)
        o = gb[..., 0:1] * o_cmp + gb[..., 1:2] * o_slc + gb[..., 2:3] * o_win
        return o.reshape(b, Q_BLOCK, NSA_D)

    out = lax.map(query_block, jnp.arange(t // Q_BLOCK))
    out = jnp.moveaxis(out, 0, 1).reshape(b, t, NSA_D)
    return (out * jax.nn.silu(z)) @ w_out


def setup_inputs(seed: int = 0) -> dict:
    key = jax.random.key(seed)
    ks = jax.random.split(key, 24)
    f32 = jnp.float32
    hd = NSA_HEAD_DIM

    def nrm(k, shape, scale):
        return jax.random.normal(k, shape, f32) * scale

    def gain(k, n):
        return 1.0 + 0.05 * jax.random.normal(k, (n,), f32)

    dt0 = jnp.exp(jax.random.uniform(ks[5], (SSD_HEADS,), f32) * (math.log(0.1) - math.log(0.001))
                  + math.log(0.001))
    dt_bias = dt0 + jnp.log(-jnp.expm1(-dt0))
    return {
        'x': nrm(ks[0], (BATCH, SEQ, D_MODEL), 1.0),
        'norm_w': gain(ks[1], D_MODEL),
        'w_in': nrm(ks[2], (D_MODEL, IN_DIM), D_MODEL ** -0.5),
        'conv_w': nrm(ks[3], (SSD_CONV_K, SSD_CONV_DIM), SSD_CONV_K ** -0.5),
        'conv_b': nrm(ks[4], (SSD_CONV_DIM,), 0.01),
        'dt_bias': dt_bias,
        'a_log': jnp.log(jax.random.uniform(ks[6], (SSD_HEADS,), f32, 1.0, 16.0)),
        'd_skip': gain(ks[7], SSD_HEADS),
        'ssd_norm_w': gain(ks[8], SSD_D_INNER),
        'q_norm_w': gain(ks[9], hd),
        'k_cmp_norm_w': gain(ks[10], hd),
        'k_slc_norm_w': gain(ks[11], hd),
        'k_win_norm_w': gain(ks[12], hd),
        'cmp_pe_k': nrm(ks[13], (CMP_BLOCK, hd), 0.02),
        'cmp_w1_k': nrm(ks[14], (CMP_BLOCK * hd, hd), (CMP_BLOCK * hd) ** -0.5),
        'cmp_b1_k': nrm(ks[15], (hd,), 0.01),
        'cmp_w2_k': nrm(ks[16], (hd, hd), hd ** -0.5),
        'cmp_pe_v': nrm(ks[17], (CMP_BLOCK, hd), 0.02),
        'cmp_w1_v': nrm(ks[18], (CMP_BLOCK * hd, hd), (CMP_BLOCK * hd) ** -0.5),
        'cmp_b1_v': nrm(ks[19], (hd,), 0.01),
        'cmp_w2_v': nrm(ks[20], (hd, hd), hd ** -0.5),
        'w_out_ssd': nrm(ks[21], (SSD_D_INNER, D_MODEL), SSD_D_INNER ** -0.5),
        'w_out_nsa': nrm(ks[22], (NSA_D, D_MODEL), NSA_D ** -0.5),
        'w_o': nrm(ks[23], (D_MODEL, D_MODEL), D_MODEL ** -0.5),
    }


def reference(x, norm_w, w_in, conv_w, conv_b, dt_bias, a_log, d_skip, ssd_norm_w,
              q_norm_w, k_cmp_norm_w, k_slc_norm_w, k_win_norm_w,
              cmp_pe_k, cmp_w1_k, cmp_b1_k, cmp_w2_k,
              cmp_pe_v, cmp_w1_v, cmp_b1_v, cmp_w2_v,
              w_out_ssd, w_out_nsa, w_o):
    sizes = [SSD_D_INNER, SSD_CONV_DIM, SSD_HEADS,
             NSA_D, NSA_KV_D, NSA_KV_D, NSA_KV_D, NSA_KV_D, NSA_KV_D, NSA_KV_D,
             NSA_D, NSA_HEADS * N_NSA_BRANCHES, D_MODEL, D_MODEL]
    offs = [int(o) for o in np.cumsum(sizes)[:-1]]
    for _ in range(DEPTH):
        h = rms_norm(x, norm_w)
        proj = h @ w_in
        (z_ssd, xbc, dt_raw, q, k_cmp, v_cmp, k_slc, v_slc, k_win, v_win,
         z_nsa, nsa_gate_logits, gl_ssd, gl_nsa) = jnp.split(proj, offs, axis=-1)
        y_ssd = ssd_branch(z_ssd, xbc, dt_raw, conv_w, conv_b, dt_bias, a_log, d_skip,
                           ssd_norm_w, w_out_ssd)
        y_nsa = nsa_branch(q, k_cmp, v_cmp, k_slc, v_slc, k_win, v_win, z_nsa, nsa_gate_logits,
                           q_norm_w, k_cmp_norm_w, k_slc_norm_w, k_win_norm_w,
                           cmp_pe_k, cmp_w1_k, cmp_b1_k, cmp_w2_k,
                           cmp_pe_v, cmp_w1_v, cmp_b1_v, cmp_w2_v, w_out_nsa)
        merged = jax.nn.sigmoid(gl_ssd) * y_ssd + jax.nn.sigmoid(gl_nsa) * y_nsa
        x = x + merged @ w_o
    return x
```

```python
import numpy as np
from contextlib import ExitStack
import concourse.bass as bass
import concourse.mybir as mybir
from concourse.bass_utils import run_bass_kernel_spmd

F32 = mybir.dt.float32
BF16 = mybir.dt.bfloat16
AF = mybir.ActivationFunctionType
ALU = mybir.AluOpType

T = 2048
D = 2048
NS = 2
NCORES = 8
IN_DIM = 14928
EPS = 1e-6
O_ZS, O_XBC, O_DT, O_Q = 0, 2048, 5120, 5152
O_KC, O_VC, O_KS, O_VS, O_KW, O_VW = 7200, 7456, 7712, 7968, 8224, 8480
O_ZN, O_GL, O_GS, O_GN = 8736, 10784, 10832, 12880


class _Op:
    __slots__ = ("eng", "fn", "deps", "dma", "sig", "val", "sem")


class Prog:
    ENGS = ("pe", "act", "dve", "pool", "sp")
    R = 8

    def __init__(self, nc, es):
        self.nc = nc
        self.streams = {e: [] for e in self.ENGS}
        self.state = {}
        self.csem = {e: es.enter_context(nc.semaphore("c_" + e)) for e in self.ENGS}
        self.ccount = {e: 0 for e in self.ENGS}
        self.dsem = {e: [es.enter_context(nc.semaphore("d_%s%d" % (e, i))) for i in range(self.R)]
                     for e in ("sp", "pool", "act")}
        self.dcount = {e: 0 for e in ("sp", "pool", "act")}
        self.waited = {e: {} for e in self.ENGS}
        self.out_dma = []

    def add(self, eng, fn, r=(), w=(), dma=False):
        op = _Op()
        op.eng, op.fn, op.dma, op.sig, op.val, op.sem = eng, fn, dma, False, 0, None
        pr = [k for k in r if k.startswith("ps") or k.startswith("seg") or k.startswith("cbp")]
        if pr:
            r = [k for k in r if k not in pr]
            w = list(w) + pr
        deps = []
        for k in r:
            st = self.state.get(k)
            if st:
                deps.extend(st[0])
        for k in w:
            st = self.state.get(k)
            if st:
                deps.extend(st[0])
                deps.extend(st[1])
        if eng == "pe" and not dma:
            deps = [d for d in deps if d.dma or d.eng != "pe"]
        op.deps = set(deps)
        self.streams[eng].append(op)
        for k in r:
            st = self.state.setdefault(k, [[], []])
            if not dma:
                st[1] = [o for o in st[1] if o.dma or o.eng != eng]
            st[1].append(op)
        for k in w:
            self.state[k] = [[op], []]
        return op

    def mm(self, out, lhsT, rhs, start=True, stop=True, r=(), w=()):
        return self.add("pe", lambda e: e.matmul(out, lhsT, rhs, start=start, stop=stop), r, w)

    def tr(self, out, in_, ident, r=(), w=()):
        return self.add("pe", lambda e: e.transpose(out, in_, ident), r, w)

    def act(self, out, in_, func, r=(), w=(), bias=None, scale=None, accum_out=None):
        kw = {}
        if bias is not None:
            kw["bias"] = bias
        if scale is not None:
            kw["scale"] = scale
        if accum_out is not None:
            kw["accum_out"] = accum_out
        return self.add("act", lambda e: e.activation(out, in_, func, **kw), r, w)

    def ts(self, out, in0, s1, s2, op0, op1=None, r=(), w=(), eng="dve"):
        if op1 is None:
            return self.add(eng, lambda e: e.tensor_scalar(out, in0, s1, None, op0), r, w)
        return self.add(eng, lambda e: e.tensor_scalar(out, in0, s1, s2, op0, op1), r, w)

    def tt(self, out, in0, in1, op, r=(), w=(), eng="dve"):
        return self.add(eng, lambda e: e.tensor_tensor(out, in0, in1, op), r, w)

    def stt(self, out, in0, scalar, in1, op0, op1, r=(), w=()):
        return self.add("dve", lambda e: e.scalar_tensor_tensor(out, in0, scalar, in1, op0, op1), r, w)

    def copy(self, out, in_, r=(), w=(), eng="dve"):
        if eng == "act":
            return self.add("act", lambda e: e.activation(out, in_, AF.Copy), r, w)
        return self.add(eng, lambda e: e.tensor_copy(out, in_), r, w)

    def memset(self, ap, val, w=(), eng="dve"):
        return self.add(eng, lambda e: e.memset(ap, val), (), w)

    def dma(self, out, in_, r=(), w=(), q="sp", is_out=False, slow=False):
        if slow:
            op = self.add(q, lambda e: e.dma_start(out=out, in_=in_, allow_slow_non_contiguous=True), r, w,
                          dma=True)
        else:
            op = self.add(q, lambda e: e.dma_start(out=out, in_=in_), r, w, dma=True)
        if is_out:
            self.out_dma.append(op)
        return op

    def flush(self):
        nc = self.nc
        for e in self.ENGS:
            ops = self.streams[e]
            for op in ops:
                for d in op.deps:
                    d.sig = True
            for op in reversed(ops):
                if not op.dma:
                    op.sig = True
                    break
        for e in self.ENGS:
            for op in self.streams[e]:
                if op.dma:
                    j = self.dcount[e]
                    self.dcount[e] += 1
                    op.sem = self.dsem[e][j % self.R]
                    op.val = 16 * (j // self.R + 1)
                elif op.sig:
                    self.ccount[e] += 1
                    op.sem = self.csem[e]
                    op.val = self.ccount[e]
        final_targets = {}
        for e in self.ENGS:
            for op in self.streams[e]:
                if op.sem is not None:
                    key = id(op.sem)
                    if key not in final_targets or final_targets[key][1] < op.val:
                        final_targets[key] = (op.sem, op.val)

        def run(e):
            def body(eng):
                waited = self.waited[e]
                for op in self.streams[e]:
                    need = {}
                    for d in op.deps:
                        k = id(d.sem)
                        if k not in need or need[k][1] < d.val:
                            need[k] = (d.sem, d.val)
                    if op.dma and op.val > 16:
                        k = id(op.sem)
                        v = op.val - 16
                        if k not in need or need[k][1] < v:
                            need[k] = (op.sem, v)
                    for k, (sem, v) in need.items():
                        if waited.get(k, 0) < v:
                            eng.wait_ge(sem, v)
                            waited[k] = v
                    ins = op.fn(eng)
                    if op.dma:
                        ins.then_inc(op.sem, 16)
                    elif op.sig:
                        ins.then_inc(op.sem, 1)
                for k, (sem, v) in final_targets.items():
                    if waited.get(k, 0) < v:
                        eng.wait_ge(sem, v)
                        waited[k] = v
            return body

        with nc.Block() as block:
            block.tensor(run("pe"))
            block.scalar(run("act"))
            block.vector(run("dve"))
            block.gpsimd(run("pool"))
            block.sync(run("sp"))
        self.streams = {e: [] for e in self.ENGS}
        self.state = {}


def _consts():
    c = {}
    i = np.arange(128)
    c["ident"] = np.eye(128, dtype=np.float32)
    c["ones"] = np.ones((128, 128), np.float32)
    c["tri_le"] = (i[:, None] <= i[None, :]).astype(np.float32)
    c["strict"] = (i[:, None] > i[None, :]).astype(np.float32)
    tq = np.arange(512)
    wm = np.zeros((128, 8, 512), np.float32)
    for k in range(8):
        lo = i[:, None] + 128 * k - 512
        wm[:, k, :] = ((tq[None, :] >= lo) & (tq[None, :] < lo + 512)).astype(np.float32)
    c["wmask"] = wm
    cm = np.zeros((128, 4, 512), np.float32)
    for k in range(4):
        cm[:, k, :] = ((128 * k + i[:, None]) <= tq[None, :]).astype(np.float32)
    c["cmask"] = cm
    t = np.arange(T)
    cc = np.arange(128)
    cmp_m = ((cc[:, None] * 16 + 31) <= t[None, :]).astype(np.float32)
    cmp_m[127, :] = 0.0
    c["cmpmask"] = cmp_m
    ci = np.arange(127)[:, None] * 16
    sj = np.arange(32)[None, :] * 64
    sm = np.zeros((128, 32), np.float32)
    sm[:127] = ((ci < sj + 64) & (ci + 32 > sj)).astype(np.float32)
    c["selmap"] = sm
    j = np.arange(32)
    valid = (j[None, :] * 64 <= t[:, None])
    forced = (j[None, :] == (t // 64)[:, None]) | (j[None, :] == 0)
    vnf = (valid & ~forced).astype(np.float32)
    addc = np.where(forced, 1e9, np.where(valid, 0.0, -1.0)).astype(np.float32)
    c["vnf"] = vnf.reshape(16, 128, 32).transpose(1, 0, 2).copy()
    c["addc"] = addc.reshape(16, 128, 32).transpose(1, 0, 2).copy()
    ex = np.zeros((32, 16, 128), np.float32)
    for kt in range(16):
        ex[2 * kt, kt, :64] = 1.0
        ex[2 * kt + 1, kt, 64:] = 1.0
    c["expand"] = ex
    return c


def _bc_free(ap, shape):
    return ap.to_broadcast(shape)


class Builder:
    def __init__(self, dbg=None, nseq=NS, phases=("norm", "inproj", "ssd", "nsa", "out")):
        self.dbg = dbg or set()
        self.nseq = nseq
        self.phases = phases
        self.nc = bass.Bass("TRN2", target_bir_lowering=False)
        self.cvals = _consts()

    def declare(self):
        nc = self.nc
        inp = lambda n, s: nc.dram_tensor(n, list(s), F32, kind="ExternalInput").ap()
        self.x = inp("x", (NS, T, D))
        self.norm_w = inp("norm_w", (D,))
        self.w_in = inp("w_in", (D, IN_DIM))
        self.conv_w = inp("conv_w", (4, 3072))
        self.conv_b = inp("conv_b", (3072,))
        self.dt_bias = inp("dt_bias", (32,))
        self.a_log = inp("a_log", (32,))
        self.d_skip = inp("d_skip", (32,))
        self.ssd_norm_w = inp("ssd_norm_w", (D,))
        self.q_norm_w = inp("q_norm_w", (128,))
        self.k_cmp_norm_w = inp("k_cmp_norm_w", (128,))
        self.k_slc_norm_w = inp("k_slc_norm_w", (128,))
        self.k_win_norm_w = inp("k_win_norm_w", (128,))
        self.cmp_pe = [inp("cmp_pe_k", (32, 128)), inp("cmp_pe_v", (32, 128))]
        self.cmp_w1 = [inp("cmp_w1_k", (4096, 128)), inp("cmp_w1_v", (4096, 128))]
        self.cmp_b1 = [inp("cmp_b1_k", (128,)), inp("cmp_b1_v", (128,))]
        self.cmp_w2 = [inp("cmp_w2_k", (128, 128)), inp("cmp_w2_v", (128, 128))]
        self.w_out_ssd = inp("w_out_ssd", (D, D))
        self.w_out_nsa = inp("w_out_nsa", (D, D))
        self.w_o = inp("w_o", (D, D))
        self.cd = {k: inp("c_" + k, v.shape) for k, v in self.cvals.items()}
        self.out = nc.dram_tensor("out", [NS, T, D], F32, kind="ExternalOutput").ap()
        skind = "ExternalOutput" if self.dbg else "Internal"
        sc = lambda n, s, dt=BF16: nc.dram_tensor(n, list(s), dt, kind=skind).ap()
        self.s_zs = sc("s_zs", (NS, T, D))
        self.s_zn = sc("s_zn", (NS, T, D))
        self.s_xs = sc("s_xs", (NS, T, 2560))
        self.s_BC = sc("s_BC", (NS, 8, 128, T))
        self.s_qT = sc("s_qT", (NS, 16, 128, T))
        self.s_gls = sc("s_gls", (NS, D, T))
        self.s_gln = sc("s_gln", (NS, D, T))
        self.s_ynT = sc("s_ynT", (NS, D, T))
        self.s_onT = sc("s_onT", (NS, D, T))
        self.dbg_t = {}

    def dbg_out(self, name, shape, dtype=F32):
        t = self.nc.dram_tensor("dbg_" + name, list(shape), dtype, kind="ExternalOutput").ap()
        self.dbg_t[name] = t
        return t

    @staticmethod
    def pipeline(jobs, depth=1):
        n = len(jobs)
        for j in range(n + depth):
            if j < n:
                jobs[j][0]()
            if j >= depth:
                jobs[j - depth][1]()

    def sb(self, es, name, shape, dtype):
        self._uid = getattr(self, "_uid", 0) + 1
        return es.enter_context(self.nc.sbuf_tensor("%s_%d" % (name, self._uid), list(shape), dtype))

    def build(self):
        nc = self.nc
        self.declare()
        with ExitStack() as es:
            P = self.P = Prog(nc, es)
            self.psbig = es.enter_context(nc.psum_tensor("psbig", [128, 4096], F32))
            self.ps = [self.psbig[:, i * 512:(i + 1) * 512] for i in range(8)]
            self.alloc_persistent(es)
            self.load_consts()
            P.flush()
            for s in range(self.nseq):
                self.seq(s)
        return nc

    def alloc_persistent(self, es):
        sb = self.sb
        self.ident_f = sb(es, "ident_f", (128, 128), F32)
        self.ident_b = sb(es, "ident_b", (128, 128), BF16)
        self.ones_f = sb(es, "ones_f", (128, 128), F32)
        self.ones_b = sb(es, "ones_b", (128, 128), BF16)
        self.tri_le = sb(es, "tri_le", (128, 128), F32)
        self.strict = sb(es, "strict", (128, 128), F32)
        self.eps_t = sb(es, "eps_t", (128, 1), F32)
        self.nwT = sb(es, "nwT", (128, 16), F32)
        self.snwT = sb(es, "snwT", (128, 16), F32)
        self.cwT = sb(es, "cwT", (128, 24, 4), F32)
        self.cbT = sb(es, "cbT", (128, 24), F32)
        self.dtb = sb(es, "dtb", (128, 32), F32)
        self.a_bc = sb(es, "a_bc", (128, 32), F32)
        self.dsk = sb(es, "dsk", (128, 32), F32)
        self.qw = sb(es, "qw", (128, 4), F32)
        self.cmp_w2b = sb(es, "cmp_w2b", (128, 2, 128), BF16)
        self.cmp_bias = sb(es, "cmp_bias", (128, 2), F32)
        self.selmap_f = sb(es, "selmap_f", (128, 32), F32)

    def alloc_seq(self, es):
        sb, P = self.sb, self.P
        self.kTs = sb(es, "kTs", (128, 2, T), BF16)
        self.kTw = sb(es, "kTw", (128, 2, T), BF16)
        self.vs = sb(es, "vs", (128, 16, 2, 129), BF16)
        self.vw = sb(es, "vw", (128, 16, 2, 129), BF16)
        self.kcT = sb(es, "kcT", (128, 2, 128), BF16)
        self.vc = sb(es, "vc", (128, 2, 161), BF16)
        self.dt = sb(es, "dt", (128, 16, 32), F32)
        self.sg = sb(es, "sg", (128, 16, 48), F32)
        P.memset(self.vs[:, :, :, 128:129], 1.0, w=["vs"])
        P.memset(self.vw[:, :, :, 128:129], 1.0, w=["vw"])
        P.memset(self.vc[:, :, 128:129], 1.0, w=["vc"])
        for g in range(2):
            P.copy(self.vc[:, g, 129:161], self.selmap_f[:], r=["selmap_f"], w=["vc"])

    def load_consts(self):
        P = self.P
        cd = self.cd
        P.dma(self.ident_f[:], cd["ident"], w=["ident_f"])
        P.dma(self.ones_f[:], cd["ones"], w=["ones_f"])
        P.dma(self.tri_le[:], cd["tri_le"], w=["tri_le"])
        P.dma(self.strict[:], cd["strict"], w=["strict"])
        P.copy(self.ident_b[:], self.ident_f[:], r=["ident_f"], w=["ident_b"])
        P.copy(self.ones_b[:], self.ones_f[:], r=["ones_f"], w=["ones_b"])
        P.memset(self.eps_t[:], EPS, w=["eps_t"])
        P.dma(self.nwT[:], self.norm_w.rearrange("(k p) -> p k", p=128), w=["nwT"], slow=True)
        P.dma(self.snwT[:], self.ssd_norm_w.rearrange("(k p) -> p k", p=128), w=["snwT"], slow=True)
        for k in range(4):
            P.dma(self.cwT[:, :, k], self.conv_w[k].rearrange("(b p) -> p b", p=128), w=["cwT"], slow=True)
        P.dma(self.cbT[:], self.conv_b.rearrange("(b p) -> p b", p=128), w=["cbT"], slow=True)
        P.dma(self.dtb[:], self.dt_bias.partition_broadcast(128), w=["dtb"])
        P.dma(self.a_bc[:], self.a_log.partition_broadcast(128), w=["a_bc"])
        P.dma(self.dsk[:], self.d_skip.partition_broadcast(128), w=["dsk"])
        for i, wv in enumerate([self.q_norm_w, self.k_cmp_norm_w, self.k_slc_norm_w, self.k_win_norm_w]):
            P.dma(self.qw[:, i:i + 1], wv.rearrange("(p o) -> p o", o=1), w=["qw"], slow=True)
        P.act(self.a_bc[:], self.a_bc[:], AF.Exp, r=["a_bc"], w=["a_bc"])
        P.ts(self.a_bc[:], self.a_bc[:], -1.0, None, ALU.mult, r=["a_bc"], w=["a_bc"])
        P.ts(self.qw[:, 0:1], self.qw[:, 0:1], float(128 ** -0.5), None, ALU.mult, r=["qw"], w=["qw"])
        P.dma(self.selmap_f[:], cd["selmap"], w=["selmap_f"])
        for kv in range(2):
            P.dma(self.tri_le[:], self.cmp_w2[kv], r=["tri_le"], w=["tri_le"])
            P.copy(self.cmp_w2b[:, kv, :], self.tri_le[:], r=["tri_le"], w=["cmp_w2b"])
        P.dma(self.tri_le[:], cd["tri_le"], r=["tri_le"], w=["tri_le"])

    def seq(self, s):
        P = self.P
        ph = self.phases
        with ExitStack() as eseq:
            self.alloc_seq(eseq)
            with ExitStack() as es:
                hT = self.sb(es, "hT", (128, 16, T), BF16)
                if "norm" in ph:
                    with ExitStack() as e1:
                        self.phase_norm(e1, s, hT)
                        P.flush()
                if "inproj" in ph:
                    with ExitStack() as e2:
                        self.phase_inproj(e2, s, hT)
                        P.flush()
            if "ssd" in ph:
                with ExitStack() as e3:
                    self.phase_ssd(e3, s)
                    P.flush()
            if "nsa" in ph:
                with ExitStack() as e4:
                    self.phase_nsa(e4, s)
                    P.flush()
            P.flush()
        if "out" in ph:
            with ExitStack() as e5:
                self.phase_out(e5, s)
                P.flush()

    def phase_norm(self, es, s, hT):
        P, sb = self.P, self.sb
        xt = [sb(es, "xt%d" % i, (128, D), F32) for i in range(2)]
        xn = [sb(es, "xn%d" % i, (128, D), BF16) for i in range(2)]
        junk = sb(es, "junk", (128, D), BF16)
        ss = sb(es, "ss", (128, 2), F32)
        rs = sb(es, "rs", (128, 2), F32)
        for tt in range(16):
            i = tt % 2
            P.dma(xt[i][:], self.x[s, tt * 128:(tt + 1) * 128, :], w=["xt%d" % i])
            P.act(junk[:], xt[i][:], AF.Square, r=["xt%d" % i], w=["junk", "ss%d" % i],
                  accum_out=ss[:, i:i + 1])
            P.act(rs[:, i:i + 1], ss[:, i:i + 1], AF.Sqrt, r=["ss%d" % i, "eps_t"], w=["rs%d" % i],
                  bias=self.eps_t[:, 0:1], scale=1.0 / D)
            P.add("dve", lambda e, o=rs[:, i:i + 1]: e.reciprocal(o, o), r=["rs%d" % i], w=["rs%d" % i])
            P.ts(xn[i][:], xt[i][:], rs[:, i:i + 1], None, ALU.mult,
                 r=["xt%d" % i, "rs%d" % i], w=["xn%d" % i])
            for half in range(2):
                b = 2 * i + half
                pb = self.ps[b][:].bitcast(BF16)
                for j in range(8):
                    kc = half * 8 + j
                    P.tr(pb[:, j * 128:(j + 1) * 128], xn[i][:, kc * 128:(kc + 1) * 128], self.ident_b[:],
                         r=["xn%d" % i, "ident_b"], w=["ps%d" % b])
                P.tt(hT[:, half * 8:(half + 1) * 8, tt * 128:(tt + 1) * 128],
                     pb.rearrange("p (k t) -> p k t", k=8),
                     self.nwT[:, half * 8:(half + 1) * 8].unsqueeze(2).to_broadcast([128, 8, 128]),
                     ALU.mult, r=["ps%d" % b, "nwT"], w=["hT%d" % tt])
        if "hT" in self.dbg and s == 0:
            d = self.dbg_out("hT", (128, 16, T), BF16)
            P.dma(d, hT[:], r=["hT%d" % t for t in range(16)], w=["dbg_hT"])

    def phase_inproj(self, es, s, hT):
        P, sb, ps = self.P, self.sb, self.ps
        CW = 256
        wst = [sb(es, "wst0", (128, 16, CW), F32)] * 2
        wbf = [sb(es, "wbf%d" % i, (128, 16, CW), BF16) for i in range(2)]
        stg = [sb(es, "stg%d" % i, (128, 4096), BF16) for i in range(2)]
        raw = sb(es, "craw", (128, T + 4), F32)
        acc = sb(es, "cacc", (128, T), F32)
        xcb = sb(es, "cxcb", (128, T), BF16)
        sq = [sb(es, "sq%d" % i, (128, 512), BF16) for i in range(2)]
        rstd = [sb(es, "rstd%d" % i, (128, 512), F32) for i in range(2)]
        dtraw = sb(es, "dtraw", (128, 16, 80), F32)
        tmp80 = sb(es, "tmp80", (128, 16, 32), F32)
        tmp80b = sb(es, "tmp80b", (128, 16, 32), F32)
        hid = sb(es, "hid", (128, 128), BF16)
        w1b = sb(es, "w1b", (128, 32, 128), BF16)
        pes = sb(es, "pes", (32, 128), F32)
        peT = sb(es, "peT", (128, 34), BF16)
        b1s = sb(es, "b1s", (128, 1), F32)
        one_t = sb(es, "one_t", (128, 1), F32)
        P.memset(one_t[:], 1.0, w=["one_t"])
        win3 = self.w_in.rearrange("(k p) n -> p k n", p=128)
        st = {"bank": 0, "stg": 0, "nrm": 0}
        tasks = []

        def load_block(idx, segs):
            slot = idx % 2
            off = 0
            for (c0, cw) in segs:
                P.dma(wst[slot][:, :, off:off + cw], win3[:, :, c0:c0 + cw], w=["wst0"])
                off += cw
            P.copy(wbf[slot][:, :, 0:off], wst[slot][:, :, 0:off], r=["wst0"], w=["wbf%d" % slot],
                   eng="pool")

        def nbank():
            b = st["bank"] % 4
            st["bank"] += 1
            return b

        def gemm_fm(slot, j0, post, defer=False):
            pend = None
            for tc in range(4):
                b = nbank()
                for kc in range(16):
                    P.mm(ps[b][:, :], wbf[slot][:, kc, j0:j0 + 128], hT[:, kc, tc * 512:(tc + 1) * 512],
                         start=(kc == 0), stop=(kc == 15), r=["wbf%d" % slot], w=["ps%d" % b])
                if defer:
                    if pend is not None:
                        post(*pend)
                    pend = (tc, b)
                else:
                    post(tc, b)
            if pend is not None:
                post(*pend)

        def gemm_tm(slot, cw, post):
            for tt in range(16):
                b = nbank()
                for kc in range(16):
                    P.mm(ps[b][:, 0:cw], hT[:, kc, tt * 128:(tt + 1) * 128], wbf[slot][:, kc, 0:cw],
                         start=(kc == 0), stop=(kc == 15), r=["wbf%d" % slot], w=["ps%d" % b])
                post(tt, b)

        def nstg():
            i = st["stg"] % 2
            st["stg"] += 1
            return i

        def rms_fm(b, ncols, wcol, dest, dkey):
            i = st["nrm"] % 2
            st["nrm"] += 1
            P.act(sq[i][:, 0:ncols], ps[b][:, 0:ncols], AF.Square, r=["ps%d" % b], w=["sq%d" % i])
            b2 = 4 + i
            P.mm(ps[b2][:, 0:ncols], self.ones_b[:], sq[i][:, 0:ncols], r=["sq%d" % i, "ones_b"], w=["ps%d" % b2])
            P.act(rstd[i][:, 0:ncols], ps[b2][:, 0:ncols], AF.Ln, r=["ps%d" % b2, "eps_t"], w=["rstd%d" % i],
                  bias=self.eps_t[:, 0:1], scale=1.0 / 128)
            P.act(rstd[i][:, 0:ncols], rstd[i][:, 0:ncols], AF.Exp, r=["rstd%d" % i], w=["rstd%d" % i], scale=-0.5)
            P.stt(dest, ps[b][:, 0:ncols], self.qw[:, wcol:wcol + 1], rstd[i][:, 0:ncols], ALU.mult, ALU.mult,
                  r=["ps%d" % b, "rstd%d" % i, "qw"], w=[dkey])

        def t_small(slot):
            def post_small(tt, b):
                P.copy(dtraw[:, tt, :], ps[b][:, 0:80], r=["ps%d" % b], w=["dtraw"])
            gemm_tm(slot, 80, post_small)
            xdt = self.dt
            P.tt(xdt[:], dtraw[:, :, 0:32], self.dtb[:].unsqueeze(1).to_broadcast([128, 16, 32]), ALU.add,
                 r=["dtraw", "dtb"], w=["dt"])
            P.ts(tmp80b[:], xdt[:], -1.0, None, ALU.mult, r=["dt"], w=["tmp80b"])
            P.tt(tmp80[:], xdt[:], tmp80b[:], ALU.min, r=["dt", "tmp80b"], w=["tmp80"])
            P.act(tmp80[:], tmp80[:], AF.Exp, r=["tmp80"], w=["tmp80"])
            P.act(tmp80[:], tmp80[:], AF.Ln, r=["tmp80", "one_t"], w=["tmp80"], bias=one_t[:, 0:1], scale=1.0)
            P.ts(tmp80b[:], xdt[:], 0.0, None, ALU.max, r=["dt"], w=["tmp80b"])
            P.tt(xdt[:], tmp80[:], tmp80b[:], ALU.add, r=["tmp80", "tmp80b"], w=["dt"])
            P.act(self.sg[:], dtraw[:, :, 32:80], AF.Sigmoid, r=["dtraw"], w=["sg"])
        tasks.append(([(O_DT, 32), (O_GL, 48)], t_small))

        for (o_v, vt, vkey) in ((O_VS, self.vs, "vs"), (O_VW, self.vw, "vw")):
            def t_v(slot, vt=vt, vkey=vkey):
                def post_v(tt, b):
                    P.copy(vt[:, tt, :, 0:128], ps[b][:, 0:256].rearrange("p (g d) -> p g d", g=2),
                           r=["ps%d" % b], w=[vkey])
                gemm_tm(slot, 256, post_v)
            tasks.append(([(o_v, 256)], t_v))

        for (o_k, kt_, kkey, wcol) in ((O_KS, self.kTs, "kTs", 2), (O_KW, self.kTw, "kTw", 3)):
            def t_k(slot, kt_=kt_, kkey=kkey, wcol=wcol):
                for g in range(2):
                    gemm_fm(slot, g * 128,
                            lambda tc, b, g=g: rms_fm(b, 512, wcol, kt_[:, g, tc * 512:(tc + 1) * 512], kkey),
                            defer=True)
            tasks.append(([(o_k, 256)], t_k))

        for hb in range(8):
            def t_q(slot, hb=hb):
                for hh in range(2):
                    h = hb * 2 + hh
                    si = nstg()
                    gemm_fm(slot, hh * 128,
                            lambda tc, b, si=si: rms_fm(b, 512, 0, stg[si][:, tc * 512:(tc + 1) * 512],
                                                        "stg%d" % si), defer=True)
                    P.dma(self.s_qT[s, h], stg[si][:, 0:T], r=["stg%d" % si], w=["s_qT"])
            tasks.append(([(O_Q + hb * 256, 256)], t_q))

        for kv, o_c in ((0, O_KC), (1, O_VC)):
            def t_c(slot, kv=kv):
                a3 = acc[:, :].rearrange("p (l j) -> p l j", j=128)
                for half in range(2):
                    P.dma(a3, self.cmp_w1[kv][half * 2048:(half + 1) * 2048, :].rearrange("(l d) j -> d l j", d=128),
                          w=["cacc"])
                    P.copy(w1b[:, half * 16:(half + 1) * 16, :], a3, r=["cacc"], w=["cmp_w1b"])
                P.dma(pes[:], self.cmp_pe[kv], w=["pes"])
                P.tr(ps[6][:, 0:32], pes[:], self.ident_f[0:32, 0:32], r=["pes", "ident_f"], w=["ps6"])
                P.memset(peT[:], 0.0, w=["peT"])
                P.copy(peT[:, 0:32], ps[6][:, 0:32], r=["ps6"], w=["peT"])
                for l in range(32):
                    P.mm(ps[7][:, 0:2], w1b[:, l, :], peT[:, l:l + 2], start=(l == 0), stop=(l == 31),
                         r=["peT", "cmp_w1b"], w=["ps7"])
                P.dma(b1s[:], self.cmp_b1[kv].rearrange("(p o) -> p o", o=1), w=["b1s"], slow=True)
                P.tt(self.cmp_bias[:, kv:kv + 1], ps[7][:, 0:1], b1s[:], ALU.add, r=["ps7", "b1s"],
                     w=["cmp_bias"])
                for g in range(2):
                    def post_raw(tc, b):
                        P.copy(xcb[:, tc * 512:(tc + 1) * 512], ps[b][:, :], r=["ps%d" % b], w=["cxcb"], eng="act")
                    gemm_fm(slot, g * 128, post_raw)
                    v3 = xcb[:, :].rearrange("p (i r) -> p i r", r=16)
                    b = 6
                    for l in range(32):
                        rhs = v3[:, 0:127, l] if l < 16 else v3[:, 1:128, l - 16]
                        P.mm(ps[b][:, 0:127], w1b[:, l, :], rhs, start=(l == 0), stop=(l == 31),
                             r=["cxcb", "cmp_w1b"], w=["ps%d" % b])
                    P.act(hid[:, 0:127], ps[b][:, 0:127], AF.Silu, r=["ps%d" % b, "cmp_bias"], w=["hid"],
                          bias=self.cmp_bias[:, kv:kv + 1])
                    b = 7
                    if kv == 0:
                        P.mm(ps[b][:, 0:127], self.cmp_w2b[:, 0, :], hid[:, 0:127], r=["hid", "cmp_w2b"],
                             w=["ps%d" % b])
                        rms_fm(b, 127, 1, self.kcT[:, g, 0:127], "kcT")
                    else:
                        P.mm(ps[b][0:127, 0:128], hid[:, 0:127], self.cmp_w2b[:, 1, :], r=["hid", "cmp_w2b"],
                             w=["ps%d" % b])
                        P.copy(self.vc[0:127, g, 0:128], ps[b][0:127, 0:128], r=["ps%d" % b], w=["vc"])
            tasks.append(([(o_c, 256)], t_c))

        for (o_g, dst) in ((O_GS, self.s_gls), (O_GN, self.s_gln)):
            for blk in range(8):
                def t_g(slot, blk=blk, dst=dst):
                    for hh in range(2):
                        si = nstg()

                        def post_g(tc, b, si=si):
                            P.act(stg[si][:, tc * 512:(tc + 1) * 512], ps[b][:, :], AF.Sigmoid,
                                  r=["ps%d" % b], w=["stg%d" % si])
                        gemm_fm(slot, hh * 128, post_g)
                        f0 = blk * 256 + hh * 128
                        P.dma(dst[s, f0:f0 + 128, :], stg[si][:, 0:T], r=["stg%d" % si], w=["s_g"])
                tasks.append(([(o_g + blk * 256, 256)], t_g))

        for (o_z, dst) in ((O_ZS, self.s_zs), (O_ZN, self.s_zn)):
            for blk in range(8):
                def t_z(slot, blk=blk, dst=dst):
                    si = nstg()

                    def post_z(tt, b):
                        P.act(stg[si][:, tt * 256:(tt + 1) * 256], ps[b][:, 0:256], AF.Silu,
                              r=["ps%d" % b], w=["stg%d" % si])
                    gemm_tm(slot, 256, post_z)
                    P.dma(dst[s].rearrange("(tt p) c -> p tt c", p=128)[:, :, blk * 256:(blk + 1) * 256],
                          stg[si][:, :].rearrange("p (tt c) -> p tt c", c=256),
                          r=["stg%d" % si], w=["s_z"])
                tasks.append(([(o_z + blk * 256, 256)], t_z))

        xpend = []

        def x_tail(cb):
            def run():
                if cb < 20:
                    si = nstg()
                    for half in range(2):
                        b = 6 + half
                        pb = ps[b][:].bitcast(BF16)
                        for j in range(8):
                            tt = half * 8 + j
                            P.tr(pb[:, j * 128:(j + 1) * 128], xcb[:, tt * 128:(tt + 1) * 128],
                                 self.ident_b[:], r=["cxcb", "ident_b"], w=["ps%d" % b])
                        P.copy(stg[si][:, half * 1024:(half + 1) * 1024], pb, r=["ps%d" % b],
                               w=["stg%d" % si])
                    P.dma(self.s_xs[s].rearrange("(tt p) c -> p tt c", p=128)[:, :, cb * 128:(cb + 1) * 128],
                          stg[si][:, 0:T].rearrange("p (tt c) -> p tt c", c=128),
                          r=["stg%d" % si], w=["s_xs"])
                if cb >= 16:
                    P.dma(self.s_BC[s, cb - 16], xcb[:, 0:T], r=["cxcb"], w=["s_BC"])
            return run

        for blk in range(12):
            def t_x(slot, blk=blk):
                for hh in range(2):
                    cb = blk * 2 + hh

                    def post_c(tc, b):
                        P.copy(raw[:, 3 + tc * 512:3 + (tc + 1) * 512], ps[b][:, :], r=["ps%d" % b], w=["craw"],
                               eng="act")
                    P.memset(raw[:, 0:3], 0.0, w=["craw"])
                    gemm_fm(slot, hh * 128, post_c)
                    if xpend:
                        xpend.pop()()
                    P.ts(acc[:], raw[:, 3:3 + T], self.cwT[:, cb, 3:4], self.cbT[:, cb:cb + 1], ALU.mult, ALU.add,
                         r=["craw", "cwT", "cbT"], w=["cacc"])
                    for k in range(3):
                        P.stt(acc[:], raw[:, k:k + T], self.cwT[:, cb, k:k + 1], acc[:], ALU.mult, ALU.add,
                              r=["craw", "cacc", "cwT"], w=["cacc"])
                    P.act(xcb[:], acc[:], AF.Silu, r=["cacc"], w=["cxcb"])
                    xpend.append(x_tail(cb))
                if blk == 11:
                    xpend.pop()()
            tasks.append(([(O_XBC + blk * 256, 256)], t_x))

        sel = getattr(self, "task_sel", None)
        if sel is not None:
            tasks = [tasks[i] for i in sel]
        load_block(0, tasks[0][0])
        for i, (segs, fn) in enumerate(tasks):
            if i + 1 < len(tasks):
                load_block(i + 1, tasks[i + 1][0])
            fn(i % 2)

        if s == 0:
            for nm, tns, keys in (("kTs", self.kTs, ["kTs"]), ("kTw", self.kTw, ["kTw"]), ("vs", self.vs, ["vs"]),
                                  ("vw", self.vw, ["vw"]), ("kcT", self.kcT, ["kcT"]), ("vc", self.vc, ["vc"])):
                if nm in self.dbg:
                    d = self.dbg_out(nm, tns.shape, BF16)
                    P.dma(d, tns[:], r=keys, w=["dbg_" + nm])
            for nm, tns, keys in (("dt", self.dt, ["dt"]), ("sg", self.sg, ["sg"])):
                if nm in self.dbg:
                    d = self.dbg_out(nm, tns.shape, F32)
                    P.dma(d, tns[:], r=keys, w=["dbg_" + nm])

    def phase_ssd(self, es, s):
        P, sb, ps = self.P, self.sb, self.ps
        xsB = [sb(es, "xsB%d" % i, (128, 2560), BF16) for i in range(2)]
        bc = [sb(es, "bc%d" % i, (128, 8, 128), BF16) for i in range(2)]
        zsc = [sb(es, "zsc%d" % i, (128, D), BF16) for i in range(2)]
        xw = [sb(es, "xw%d" % i, (128, D), BF16) for i in range(2)]
        la = sb(es, "la", (128, 512), F32)
        lc = sb(es, "lc", (128, 512), F32)
        dfs = sb(es, "dfs", (128, 512), F32)
        dte = sb(es, "dte", (128, 512), F32)
        cdec = sb(es, "cdec", (128, 512), F32)
        wsc = sb(es, "wsc", (128, 512), F32)
        cbs = [sb(es, "cbs%d" % i, (128, 128), F32) for i in range(2)]
        Ah = [sb(es, "Ah%d" % i, (128, 128), F32) for i in range(6)]
        Lx = [sb(es, "Lx%d" % i, (128, 128), F32) for i in range(6)]
        MT = [sb(es, "MT%d" % i, (128, 128), BF16) for i in range(6)]
        S = sb(es, "S", (128, D), F32)
        prevT = sb(es, "prevT", (128, D), BF16)
        y = [sb(es, "y%d" % i, (128, D), F32) for i in range(2)]
        ysb = [sb(es, "ysb%d" % i, (128, 512), F32) for i in range(2)]
        osb = [sb(es, "osb%d" % i, (128, 512), F32) for i in range(2)]
        ssb = [sb(es, "ssb%d" % i, (128, 512), F32) for i in range(2)]
        tmpa = sb(es, "tmpa", (128, 512), F32)
        tmpb = sb(es, "tmpb", (128, 512), F32)
        v = sb(es, "v", (128, D), F32)
        vn = [sb(es, "vn%d" % i, (128, D), BF16) for i in range(2)]
        pending = []
        junk = sb(es, "junk3", (128, D), BF16)
        ss = sb(es, "ss3", (128, 2), F32)
        rs = sb(es, "rs3", (128, 2), F32)
        ynstg = [sb(es, "ynstg%d" % i, (128, 16, 512), BF16) for i in range(2)]
        dtf = self.dt[:].rearrange("p c h -> p (c h)")
        dmat = sb(es, "dmat", (128, 32, 128), BF16)
        for h in range(32):
            P.ts(dmat[:, h, :], self.ident_f[:], self.dsk[:, h:h + 1], None, ALU.mult, r=["ident_f", "dsk"],
                 w=["dmat"])

        def v3(ap, a, b_):
            return ap.rearrange("p (a b) -> p a b", a=a, b=b_)

        def bcast(ap2, n, m):
            return ap2.unsqueeze(2).to_broadcast([128, n, m])

        P.tt(v3(la[:, :], 16, 32), self.dt[:], self.a_bc[:].unsqueeze(1).to_broadcast([128, 16, 32]), ALU.mult,
             r=["dt", "a_bc"], w=["la"])
        P.mm(ps[3][:, :], self.tri_le[:], la[:, :], r=["la", "tri_le"], w=["ps3"])
        P.mm(ps[4][:, :], self.ones_f[:], la[:, :], r=["la", "ones_f"], w=["ps4"])
        P.copy(lc[:], ps[3][:, :], r=["ps3"], w=["lc"])
        P.act(dfs[:], ps[3][:, :], AF.Exp, r=["ps3"], w=["dfs"])
        P.tt(dte[:], ps[4][:, :], lc[:], ALU.subtract, r=["ps4", "lc"], w=["dte"])
        P.act(dte[:], dte[:], AF.Exp, r=["dte"], w=["dte"])
        P.act(cdec[:], ps[4][:, :], AF.Exp, r=["ps4"], w=["cdec"])
        P.tt(wsc[:], dte[:], dtf, ALU.mult, r=["dte", "dt"], w=["wsc"])
        P.memset(S[:], 0.0, w=["S%d" % g for g in range(4)])
        P.memset(prevT[:], 0.0, w=["prevT%d" % g for g in range(4)])
        xs3 = self.s_xs[s]
        def chunk_loads(c):
            i = c % 2
            P.dma(xsB[i][:], xs3[c * 128:(c + 1) * 128, :], w=["xsB%d" % i])
            P.dma(bc[i][:], self.s_BC[s, :, :, c * 128:(c + 1) * 128].rearrange("g n t -> n g t"), w=["bc%d" % i])
            P.dma(zsc[i][:], self.s_zs[s, c * 128:(c + 1) * 128, :], w=["zsc%d" % i])

        def chunk_xw(c):
            i = c % 2
            P.tt(v3(xw[i][:, :], 32, 64), v3(xsB[i][:, 0:D], 32, 64), bcast(wsc[:, c * 32:(c + 1) * 32], 32, 64),
                 ALU.mult, r=["xsB%d" % i, "wsc"], w=["xw%d" % i])

        chunk_loads(0)
        chunk_xw(0)
        for c in range(16):
            i = c % 2
            jobs = []
            for h in range(32):
                def s1(h=h, c=c, i=i):
                    g, k = divmod(h, 8)
                    cbk = "cbs%d" % (g % 2)
                    if k == 0:
                        cp = ps[0][:, 0:128]
                        P.mm(cp, bc[i][:, g, :], bc[i][:, 4 + g, :], r=["bc%d" % i], w=["ps0"])
                        P.tt(cbs[g % 2][:], cp, self.tri_le[:], ALU.mult, r=["ps0", "tri_le"], w=[cbk])
                    j = h % 6
                    col = c * 32 + h
                    P.ts(Ah[j][:], self.strict[:], la[:, col:col + 1], 1.0, ALU.mult, ALU.mult,
                         r=["strict", "la"], w=["Ah%d" % j], eng="pool")
                    sbk = (1, 2, 6, 7)[h % 4]
                    sp = ps[sbk][:, 0:128]
                    P.mm(sp, Ah[j][:], self.tri_le[:], r=["Ah%d" % j, "tri_le"], w=["ps%d" % sbk])
                    P.act(Lx[j][:], sp, AF.Exp, r=["ps%d" % sbk], w=["Lx%d" % j])
                    P.stt(MT[j][:], Lx[j][:], self.dt[:, c, h:h + 1], cbs[g % 2][:], ALU.mult, ALU.mult,
                          r=["Lx%d" % j, "dt", cbk], w=["MT%d" % j])

                def s2(h=h, c=c, i=i):
                    g, k = divmod(h, 8)
                    j = h % 6
                    P.mm(ps[3][:, k * 64:(k + 1) * 64], MT[j][:], xsB[i][:, h * 64:(h + 1) * 64],
                         start=True, stop=False, r=["MT%d" % j, "xsB%d" % i], w=["ps3"])
                    P.mm(ps[3][:, k * 64:(k + 1) * 64], dmat[:, h, :], xsB[i][:, h * 64:(h + 1) * 64],
                         start=False, stop=True, r=["dmat", "xsB%d" % i], w=["ps3"])
                    P.mm(ps[4][:, k * 64:(k + 1) * 64], bc[i][:, 4 + g, :], prevT[:, h * 64:(h + 1) * 64],
                         r=["bc%d" % i, "prevT%d" % g], w=["ps4"])
                    P.mm(ps[5][:, k * 64:(k + 1) * 64], xsB[i][:, D + g * 128:D + (g + 1) * 128],
                         xw[i][:, h * 64:(h + 1) * 64], r=["xsB%d" % i, "xw%d" % i], w=["ps5"])
                    if k == 7:
                        gs = slice(g * 512, (g + 1) * 512)
                        e = g % 2
                        P.copy(ysb[e][:], ps[3][:, :], r=["ps3"], w=["ysb%d" % e], eng="act")
                        P.copy(osb[e][:], ps[4][:, :], r=["ps4"], w=["osb%d" % e], eng="act")
                        P.copy(ssb[e][:], ps[5][:, :], r=["ps5"], w=["ssb%d" % e], eng="act")
                        P.tt(v3(tmpa[:, :], 8, 64), v3(osb[e][:, :], 8, 64),
                             bcast(dfs[:, c * 32 + 8 * g:c * 32 + 8 * g + 8], 8, 64),
                             ALU.mult, r=["osb%d" % e, "dfs"], w=["tmpa"])
                        P.tt(y[i][:, gs], tmpa[:], ysb[e][:], ALU.add, r=["tmpa", "ysb%d" % e], w=["y%d_%d" % (i, g)])
                        P.tt(v3(S[:, gs], 8, 64), v3(S[:, gs], 8, 64),
                             bcast(cdec[:, c * 32 + 8 * g:c * 32 + 8 * g + 8], 8, 64),
                             ALU.mult, r=["S%d" % g, "cdec"], w=["S%d" % g])
                        P.tt(S[:, gs], S[:, gs], ssb[e][:], ALU.add, r=["S%d" % g, "ssb%d" % e], w=["S%d" % g])
                        P.copy(prevT[:, gs], S[:, gs], r=["S%d" % g], w=["prevT%d" % g], eng="act")
                jobs.append((s1, s2))
                if h == 7 and pending:
                    jobs.append((lambda: None, pending.pop()))
                if h == 12 and c + 1 < 16:
                    jobs.append((lambda c=c: chunk_loads(c + 1), lambda: None))
                if h == 24 and c + 1 < 16:
                    jobs.append((lambda: None, lambda c=c: chunk_xw(c + 1)))
            self.pipeline(jobs, 4)
            ykeys = ["y%d_%d" % (i, g) for g in range(4)]
            P.tt(v[:], y[i][:], zsc[i][:], ALU.mult, r=ykeys + ["zsc%d" % i], w=["v"])
            P.act(junk[:], v[:], AF.Square, r=["v"], w=["junk3", "ss3_%d" % i], accum_out=ss[:, i:i + 1])
            P.act(rs[:, i:i + 1], ss[:, i:i + 1], AF.Ln, r=["ss3_%d" % i, "eps_t"], w=["rs3_%d" % i],
                  bias=self.eps_t[:, 0:1], scale=1.0 / D)
            P.act(rs[:, i:i + 1], rs[:, i:i + 1], AF.Exp, r=["rs3_%d" % i], w=["rs3_%d" % i], scale=-0.5)
            P.ts(vn[i][:], v[:], rs[:, i:i + 1], None, ALU.mult, r=["v", "rs3_%d" % i], w=["vn%d" % i])

            def tailB(c=c, i=i):
                q = (c // 4) % 2
                for half in range(2):
                    b = 6 + half
                    pb = ps[b][:].bitcast(BF16)
                    for jj in range(8):
                        kc = half * 8 + jj
                        P.tr(pb[:, jj * 128:(jj + 1) * 128], vn[i][:, kc * 128:(kc + 1) * 128], self.ident_b[:],
                             r=["vn%d" % i, "ident_b"], w=["ps%d" % b])
                    P.tt(ynstg[q][:, half * 8:(half + 1) * 8, (c % 4) * 128:(c % 4 + 1) * 128],
                         pb.rearrange("p (k t) -> p k t", k=8),
                         bcast(self.snwT[:, half * 8:(half + 1) * 8], 8, 128), ALU.mult,
                         r=["ps%d" % b, "snwT"], w=["ynstg%d" % q])
                if c % 4 == 3:
                    P.dma(self.s_ynT[s].rearrange("(k p) t -> p k t", p=128)[:, :, (c // 4) * 512:(c // 4 + 1) * 512],
                          ynstg[q][:], r=["ynstg%d" % q], w=["s_ynT"])
            pending.append(tailB)
        pending.pop()()

    def phase_nsa(self, es, s):
        P, sb, ps = self.P, self.sb, self.ps
        cd = self.cd
        qg = sb(es, "qg", (128, 8, T), BF16)
        znc = [sb(es, "znc%d" % i, (128, 1024), BF16) for i in range(2)]
        pexp = [sb(es, "pexp%d" % i, (128, 512), BF16) for i in range(6)]
        oacc = sb(es, "oacc", (128, 4, 1024), F32)
        imp = sb(es, "imp", (128, 4, 32), F32)
        impf = sb(es, "impf", (128, 4, 32), F32)
        max8 = sb(es, "max8", (128, 8), F32)
        sel = sb(es, "sel", (128, 32), F32)
        selT = sb(es, "selT", (128, 512), BF16)
        r1 = [sb(es, "r1_%d" % i, (128, 2), F32) for i in range(4)]
        on_bf = sb(es, "on_bf", (128, 1024), BF16)
        onstg = [sb(es, "onstg%d" % i, (128, 8, 512), BF16) for i in range(2)]
        cmpbias = sb(es, "cmpbias", (128, T), BF16)
        tribias = sb(es, "tribias", (128, 128), BF16)
        ubias = sb(es, "ubias", (128, 128), BF16)
        BIG = 30000.0
        expand = sb(es, "expand", (128, 16, 128), BF16)
        vnf = sb(es, "vnf", (128, 16, 32), F32)
        addc = sb(es, "addc", (128, 16, 32), F32)
        of = oacc[:, :, :].rearrange("p a b -> p (a b)")
        P.dma(of[:, 0:2048], cd["cmpmask"], w=["oacc"])
        P.ts(cmpbias[:, :], of[:, 0:2048], BIG, -BIG, ALU.mult, ALU.add, r=["oacc"], w=["cmpbias"])
        P.ts(tribias[:], self.tri_le[:], BIG, -BIG, ALU.mult, ALU.add, r=["tri_le"], w=["tribias"])
        P.ts(ubias[:], self.strict[:], BIG, -BIG, ALU.mult, ALU.add, r=["strict"], w=["ubias"])
        P.dma(of[0:32, 0:2048], cd["expand"].rearrange("p a b -> p (a b)"), r=["oacc"], w=["oacc"])
        P.memset(expand[:, :, :].rearrange("p a b -> p (a b)"), 0.0, w=["expand"])
        P.memset(selT[:, :], 0.0, w=["selT"])
        P.copy(expand[0:32, :, :].rearrange("p a b -> p (a b)"), of[0:32, 0:2048], r=["oacc"], w=["expand"])
        P.dma(vnf[:], cd["vnf"], w=["vnf"])
        P.dma(addc[:], cd["addc"], w=["addc"])
        st = {"b": 0, "e": 0, "c": 0, "r": 0, "z": 0, "o": 0}

        def nst():
            st["b"] += 1
            return (0, 1, 6, 7)[st["b"] % 4]

        def npx():
            st["e"] += 1
            return st["e"] % 6

        def nr():
            st["r"] += 1
            return st["r"] % 4

        accs = [sb(es, "accs%d" % i, (128, 4, 161), F32) for i in range(2)]
        r4 = [sb(es, "r4_%d" % i, (128, 4), F32) for i in range(2)]
        tmpo = [sb(es, "tmpo%d" % i, (128, 4, 128), F32) for i in range(2)]
        tmpu = sb(es, "tmpu", (128, 4, 32), F32)
        acc4 = self.psbig[:, 2 * 512:6 * 512].rearrange("p (i c) -> p i c", i=4)
        acckeys = ["ps2", "ps3", "ps4", "ps5"]

        def finish(c4, h, br, k):
            st["r"] += 1
            ai = st["r"] % 2
            ak, rk = "accs%d" % ai, "r4_%d" % ai
            ncol = 161 if br == 0 else 129
            P.copy(accs[ai][:, :, 0:ncol], acc4[:, :, 0:ncol], r=acckeys, w=[ak], eng="act")
            P.ts(r4[ai][:], accs[ai][:, :, 128], 1e-30, None, ALU.max, r=[ak], w=[rk])
            P.add("dve", lambda e, o=r4[ai][:]: e.reciprocal(o, o), r=[rk], w=[rk])
            if br == 0:
                P.tt(tmpu[:], accs[ai][:, :, 129:161], r4[ai][:].unsqueeze(2).to_broadcast([128, 4, 32]), ALU.mult,
                     r=[ak, rk], w=["tmpu"])
                P.tt(imp[:], imp[:], tmpu[:], ALU.add, r=["imp", "tmpu"], w=["imp"])
            P.tt(r4[ai][:], r4[ai][:], self.sg[:, 4 * c4:4 * c4 + 4, 3 * h + br], ALU.mult, r=[rk, "sg"], w=[rk])
            ok = "oacc_%d" % k
            for i in range(4):
                dst = oacc[:, i, k * 128:(k + 1) * 128]
                if br == 0:
                    P.ts(dst, accs[ai][:, i, 0:128], r4[ai][:, i:i + 1], None, ALU.mult, r=[ak, rk], w=[ok])
                else:
                    P.stt(dst, accs[ai][:, i, 0:128], r4[ai][:, i:i + 1], dst, ALU.mult, ALU.add,
                          r=[ak, rk, ok], w=[ok])

        for g in range(2):
            P.dma(qg[:], self.s_qT[s, 8 * g:8 * g + 8].rearrange("h d t -> d h t"), w=["qg"])
            for c4 in range(4):
                qs = slice(c4 * 512, (c4 + 1) * 512)
                nkt = 4 * c4 + 4
                P.memset(imp[:], 0.0, w=["imp"])
                jobs = []
                for k in range(8):
                    def s1(k=k):
                        b = nst()
                        P.mm(ps[b][0:127, :], self.kcT[:, g, 0:127], qg[:, k, qs], start=True, stop=False,
                             r=["kcT", "qg"], w=["ps%d" % b])
                        P.mm(ps[b][0:127, :], self.ident_b[0:127, 0:127], cmpbias[0:127, qs], start=False, stop=True,
                             r=["ident_b", "cmpbias"], w=["ps%d" % b])
                        e = npx()
                        P.act(pexp[e][0:127, :], ps[b][0:127, :], AF.Exp, r=["ps%d" % b], w=["pexp%d" % e])
                        st["cur%d" % k] = e

                    def s2(k=k):
                        e = st["cur%d" % k]
                        for i in range(4):
                            P.mm(ps[2 + i][:, 0:161], pexp[e][0:127, i * 128:(i + 1) * 128], self.vc[0:127, g, :],
                                 r=["pexp%d" % e, "vc"], w=["ps%d" % (2 + i)])
                        finish(c4, 8 * g + k, 0, k)
                    jobs.append((s1, s2))
                self.pipeline(jobs, 3)
                nkt = 4 * c4 + 4
                extra = []
                for i in range(4):
                    def t_op(i=i):
                        tt = c4 * 4 + i
                        P.tt(impf[:, i, :], imp[:, i, :], vnf[:, tt, :], ALU.mult, r=["imp", "vnf"], w=["impf%d" % i])
                        P.tt(impf[:, i, :], impf[:, i, :], addc[:, tt, :], ALU.add, r=["impf%d" % i, "addc"],
                             w=["impf%d" % i])
                        P.add("dve", lambda e, o=max8[:], a_=impf[:, i, :]: e.max(out=o, in_=a_),
                              r=["impf%d" % i], w=["max8"])
                        P.ts(sel[:], impf[:, i, :], max8[:, 7:8], None, ALU.is_ge, r=["impf%d" % i, "max8"], w=["sel"])
                        P.ts(sel[:], sel[:], BIG, -BIG, ALU.mult, ALU.add, r=["sel"], w=["sel"])
                        P.tr(ps[6][0:32, 0:128], sel[:], self.ident_f[:], r=["sel", "ident_f"], w=["ps6"])
                        P.copy(selT[0:32, i * 128:(i + 1) * 128], ps[6][0:32, 0:128], r=["ps6"], w=["selT"])
                    extra.append(t_op)
                jobs = []
                for br in (2, 1):
                    for k in range(8):
                        if br == 1:
                            kts = list(range(nkt))
                            kT_, vv, kkey, vkey = self.kTs, self.vs, "kTs", "vs"
                        else:
                            kts = list(range(max(0, 4 * c4 - 4), nkt))
                            kT_, vv, kkey, vkey = self.kTw, self.vw, "kTw", "vw"
                        for kt in kts:
                            tag = "cur_%d_%d_%d" % (k, br, kt)
                            i_lo = max(0, kt - 4 * c4)
                            i_hi = 3 if br == 1 else min(3, kt + 4 - 4 * c4)
                            c_lo, c_hi = i_lo * 128, (i_hi + 1) * 128

                            def s1(k=k, br=br, kt=kt, kT_=kT_, kkey=kkey, tag=tag, i_lo=i_lo, i_hi=i_hi,
                                   c_lo=c_lo, c_hi=c_hi):
                                b = nst()
                                pk = "ps%d" % b
                                if br == 1:
                                    diag = kt >= 4 * c4
                                    P.mm(ps[b][:, c_lo:c_hi], kT_[:, g, kt * 128:(kt + 1) * 128],
                                         qg[:, k, c4 * 512 + c_lo:c4 * 512 + c_hi], start=True, stop=False,
                                         r=[kkey, "qg"], w=[pk])
                                    P.mm(ps[b][:, c_lo:c_hi], expand[:, kt, :], selT[:, c_lo:c_hi], start=False,
                                         stop=(not diag), r=["expand", "selT"], w=[pk])
                                    if diag:
                                        P.mm(ps[b][:, c_lo:c_lo + 128], self.ident_b[:], tribias[:], start=False,
                                             stop=True, r=["ident_b", "tribias"], w=[pk])
                                else:
                                    parts = []
                                    for i in range(i_lo, i_hi + 1):
                                        qt = 4 * c4 + i
                                        if kt == qt:
                                            parts.append((i, tribias, "tribias"))
                                        elif kt == qt - 4:
                                            parts.append((i, ubias, "ubias"))
                                    P.mm(ps[b][:, c_lo:c_hi], kT_[:, g, kt * 128:(kt + 1) * 128],
                                         qg[:, k, c4 * 512 + c_lo:c4 * 512 + c_hi], start=True, stop=(not parts),
                                         r=[kkey, "qg"], w=[pk])
                                    for n_, (i, bt, bkey) in enumerate(parts):
                                        P.mm(ps[b][:, i * 128:(i + 1) * 128], self.ident_b[:], bt[:], start=False,
                                             stop=(n_ == len(parts) - 1), r=["ident_b", bkey], w=[pk])
                                e = npx()
                                P.act(pexp[e][:, c_lo:c_hi], ps[b][:, c_lo:c_hi], AF.Exp, r=[pk], w=["pexp%d" % e])
                                st[tag] = e

                            def s2(k=k, br=br, kt=kt, vv=vv, vkey=vkey, tag=tag, last=(kt == kts[-1]),
                                   i_lo=i_lo, i_hi=i_hi):
                                e = st.pop(tag)
                                for i in range(i_lo, i_hi + 1):
                                    qt = 4 * c4 + i
                                    lo = 0 if br == 1 else max(0, qt - 4)
                                    P.mm(ps[2 + i][:, 0:129], pexp[e][:, i * 128:(i + 1) * 128], vv[:, kt, g, :],
                                         start=(kt == lo), stop=(kt == qt), r=["pexp%d" % e, vkey],
                                         w=["ps%d" % (2 + i)])
                                if last:
                                    finish(c4, 8 * g + k, br, k)
                            jobs.append((s1, s2))
                            if br == 2 and extra and len(jobs) >= 2:
                                jobs.append((lambda: None, extra.pop(0)))
                    while br == 2 and extra:
                        jobs.append((lambda: None, extra.pop(0)))
                self.pipeline(jobs, 4)
                oq = st["o"] % 2
                st["o"] += 1
                for i in range(4):
                    tt = c4 * 4 + i
                    zi = st["z"] % 2
                    st["z"] += 1
                    P.dma(znc[zi][:], self.s_zn[s, tt * 128:(tt + 1) * 128, g * 1024:(g + 1) * 1024], w=["znc%d" % zi])
                    P.tt(on_bf[:], oacc[:, i, :], znc[zi][:], ALU.mult,
                         r=["oacc_%d" % k for k in range(8)] + ["znc%d" % zi], w=["on_bf"])
                    pb = ps[7][:].bitcast(BF16)
                    for k in range(8):
                        P.tr(pb[:, k * 128:(k + 1) * 128], on_bf[:, k * 128:(k + 1) * 128], self.ident_b[:],
                             r=["on_bf", "ident_b"], w=["ps7"])
                    P.copy(onstg[oq][:, :, i * 128:(i + 1) * 128], pb.rearrange("p (k t) -> p k t", k=8),
                           r=["ps7"], w=["onstg%d" % oq])
                P.dma(self.s_onT[s].rearrange("(h d) t -> d h t", d=128)[:, 8 * g:8 * g + 8, qs], onstg[oq][:],
                      r=["onstg%d" % oq], w=["s_onT"])

    def phase_out(self, es, s):
        P, sb, ps = self.P, self.sb, self.ps
        CW = 256
        wst = sb(es, "owst", (128, 16, CW), F32)
        wbf = [sb(es, "owbf%d" % i, (128, 16, CW), BF16) for i in range(2)]
        mT = sb(es, "mT", (128, 16, T), BF16)
        esA = ExitStack()
        actT = sb(esA, "actT", (128, 16, T), BF16)
        glb = [sb(esA, "glb%d" % i, (128, T), BF16) for i in range(2)]
        tmpm = [sb(esA, "tmpm%d" % i, (128, 512), F32) for i in range(2)]
        st = {"bank": 0, "gl": 0, "tm": 0}

        def nbank():
            st["bank"] += 1
            return st["bank"] % 4

        def run_stream(w2d, tasks, pre=None):
            w3 = w2d.rearrange("(k p) n -> p k n", p=128)

            def load(i):
                slot = i % 2
                P.dma(wst[:, :, :], w3[:, :, i * CW:(i + 1) * CW], w=["owst"])
                P.copy(wbf[slot][:, :, :], wst[:, :, :], r=["owst"], w=["owbf%d" % slot], eng="pool")
            load(0)
            if pre is not None:
                pre()
            for i, fn in enumerate(tasks):
                if i + 1 < len(tasks):
                    load(i + 1)
                fn(i, i % 2)

        def gemm_fm(slot, j0, act, akeys, post):
            for tc in range(4):
                b = nbank()
                for kc in range(16):
                    P.mm(ps[b][:, :], wbf[slot][:, kc, j0:j0 + 128], act[:, kc, tc * 512:(tc + 1) * 512],
                         start=(kc == 0), stop=(kc == 15), r=["owbf%d" % slot, "actT%d" % tc], w=["ps%d" % b])
                post(tc, b)

        for which, (src_T, w2d, gsrc) in enumerate(((self.s_ynT, self.w_out_ssd, self.s_gls),
                                                    (self.s_onT, self.w_out_nsa, self.s_gln))):
            def pre_act(src_T=src_T):
                for tcx in range(4):
                    P.dma(actT[:, :, tcx * 512:(tcx + 1) * 512],
                          src_T[s].rearrange("(k p) t -> p k t", p=128)[:, :, tcx * 512:(tcx + 1) * 512],
                          w=["actT%d" % tcx])
            tasks = []
            for blk in range(8):
                def t_fm(i, slot, blk=blk, which=which, gsrc=gsrc):
                    for hh in range(2):
                        fb = blk * 2 + hh
                        gi = st["gl"] % 2
                        st["gl"] += 1
                        P.dma(glb[gi][:], gsrc[s, fb * 128:(fb + 1) * 128, :], w=["glb%d" % gi])

                        def post(tc, b, fb=fb, gi=gi):
                            dst = mT[:, fb, tc * 512:(tc + 1) * 512]
                            mk = "mT%d" % fb
                            if which == 0:
                                P.tt(dst, ps[b][:, :], glb[gi][:, tc * 512:(tc + 1) * 512], ALU.mult,
                                     r=["ps%d" % b, "glb%d" % gi], w=[mk])
                            else:
                                ti = st["tm"] % 2
                                st["tm"] += 1
                                P.tt(tmpm[ti][:], ps[b][:, :], glb[gi][:, tc * 512:(tc + 1) * 512], ALU.mult,
                                     r=["ps%d" % b, "glb%d" % gi], w=["tmpm%d" % ti])
                                P.tt(dst, tmpm[ti][:], dst, ALU.add, r=["tmpm%d" % ti, mk], w=[mk])
                        gemm_fm(slot, hh * 128, actT, ["actT"], post)
                tasks.append(t_fm)
            run_stream(w2d, tasks, pre_act)

        P.flush()
        esA.close()
        xres = [sb(es, "xres%d" % i, (128, 16, CW), F32) for i in range(2)]
        ostg = [sb(es, "ostg%d" % i, (128, 16, CW), F32) for i in range(2)]
        mkeys = ["mT%d" % fb for fb in range(16)]
        tasks = []
        for blk in range(8):
            def t_tm(i, slot, blk=blk):
                xi = blk % 2
                c0 = blk * CW
                P.dma(xres[xi][:], self.x[s].rearrange("(tt p) c -> p tt c", p=128)[:, :, c0:c0 + CW],
                      w=["xres%d" % xi], q="act")
                for tt in range(16):
                    b = nbank()
                    for kc in range(16):
                        P.mm(ps[b][:, 0:CW], mT[:, kc, tt * 128:(tt + 1) * 128], wbf[slot][:, kc, :],
                             start=(kc == 0), stop=(kc == 15), r=["owbf%d" % slot] + mkeys, w=["ps%d" % b])
                    P.tt(ostg[xi][:, tt, :], ps[b][:, 0:CW], xres[xi][:, tt, :], ALU.add,
                         r=["ps%d" % b, "xres%d" % xi], w=["ostg%d" % xi])
                P.dma(self.out[s].rearrange("(tt p) c -> p tt c", p=128)[:, :, c0:c0 + CW], ostg[xi][:],
                      r=["ostg%d" % xi], w=["out"], is_out=True)
            tasks.append(t_tm)
        run_stream(self.w_o, tasks)


_W_NAMES = ["norm_w", "w_in", "conv_w", "conv_b", "dt_bias", "a_log", "d_skip", "ssd_norm_w",
            "q_norm_w", "k_cmp_norm_w", "k_slc_norm_w", "k_win_norm_w",
            "cmp_pe_k", "cmp_w1_k", "cmp_b1_k", "cmp_w2_k", "cmp_pe_v", "cmp_w1_v", "cmp_b1_v", "cmp_w2_v",
            "w_out_ssd", "w_out_nsa", "w_o"]


def _run(inputs, dbg=None, nseq=NS, phases=("norm", "inproj", "ssd", "nsa", "out"), task_sel=None, ncores=NCORES,
         trace=False):
    bld = Builder(dbg=dbg, nseq=nseq, phases=phases)
    if task_sel is not None:
        bld.task_sel = task_sel
    nc = bld.build()
    x = np.ascontiguousarray(np.asarray(inputs["x"], dtype=np.float32))
    base = {n: np.ascontiguousarray(np.asarray(inputs[n], dtype=np.float32)) for n in _W_NAMES}
    for k, v in bld.cvals.items():
        base["c_" + k] = np.ascontiguousarray(v)
    in_maps = []
    for c in range(ncores):
        m = dict(base)
        m["x"] = x[c * NS:(c + 1) * NS]
        in_maps.append(m)
    res = run_bass_kernel_spmd(nc, in_maps, core_ids=list(range(ncores)), trace=trace)
    return res


def kernel(**inputs):
    res = _run(inputs)
    out = np.concatenate([np.asarray(r["out"]) for r in res.results], axis=0)
    return out.astype(np.float32)
```
